# Optimizing a Trainium2 kernel written in Bass

```python
import jax, jax.numpy as jnp
from jax import lax
import numpy as np

D_MODEL = 1024
BATCH = 4
SEQ = 8192
DEPTH = 4

GRID_W = 64
CTX_LEN = 256
ROPE_THETA = 10000.0
NORM_EPS = 1e-6
HEAD_DIM = 64

A_HEADS = 8
A_KV_HEADS = 2
A_WINDOW = 128
A_BLOCK = 128
B_HEADS = 8
B_Q_LORA = 384
B_KV_LORA = 256
B_NOPE = 64
B_ROPE = 32
B_V = 64
B_BLOCK = 128
C_HEADS = 16
C_KH = 8
C_KW = 16

A_WIDTH = A_HEADS * HEAD_DIM
A_KV_WIDTH = A_KV_HEADS * HEAD_DIM
B_WIDTH = B_HEADS * B_V
AB_MIX = A_WIDTH + B_WIDTH
AB_SPLITS = (
    A_WIDTH,
    A_WIDTH + A_KV_WIDTH,
    A_WIDTH + 2 * A_KV_WIDTH,
    2 * A_WIDTH + 2 * A_KV_WIDTH,
    2 * A_WIDTH + 2 * A_KV_WIDTH + B_Q_LORA,
    2 * A_WIDTH + 2 * A_KV_WIDTH + B_Q_LORA + B_KV_LORA,
    2 * A_WIDTH + 2 * A_KV_WIDTH + B_Q_LORA + B_KV_LORA + B_ROPE,
)
AB_IN = 2 * A_WIDTH + 2 * A_KV_WIDTH + B_Q_LORA + B_KV_LORA + B_ROPE + B_WIDTH
C_WIDTH = C_HEADS * HEAD_DIM
C_IN = 4 * C_WIDTH
N_EVEN = (DEPTH + 1) // 2
N_ODD = DEPTH // 2

kernel_name = "hybrid_window_mla_natten_prefix_dit"


def rms_norm(x, g):
    xf = x.astype(jnp.float32)
    y = xf * lax.rsqrt(jnp.mean(xf * xf, axis=-1, keepdims=True) + NORM_EPS)
    return (y * g.astype(jnp.float32)).astype(x.dtype)


def axial_rope_angles(n_tokens, rot_dim):
    t = jnp.arange(n_tokens, dtype=jnp.int32)
    row = (t // GRID_W).astype(jnp.float32)
    col = (t % GRID_W).astype(jnp.float32)
    pairs_per_axis = rot_dim // 4
    inv = ROPE_THETA ** (-jnp.arange(pairs_per_axis, dtype=jnp.float32) / pairs_per_axis)
    ang = jnp.concatenate([row[:, None] * inv, col[:, None] * inv], axis=-1)
    return jnp.cos(ang), jnp.sin(ang)


def apply_rope(x, cos, sin):
    d = x.shape[-1]
    xr = x.reshape(x.shape[:-1] + (d // 2, 2)).astype(jnp.float32)
    x1, x2 = xr[..., 0], xr[..., 1]
    out = jnp.stack([x1 * cos - x2 * sin, x1 * sin + x2 * cos], axis=-1)
    return out.reshape(x.shape).astype(x.dtype)


def gqa_sink_dense(q, k, v, sink):
    bn, n, _, d = q.shape
    grp = A_HEADS // A_KV_HEADS
    qg = q.reshape(bn, n, A_KV_HEADS, grp, d)
    s = jnp.einsum('bqhgd,bkhd->bhgqk', qg, k).astype(jnp.float32) * (d ** -0.5)
    s_sink = jnp.broadcast_to(sink.astype(jnp.float32).reshape(A_KV_HEADS, grp)[None, :, :, None, None], s.shape[:-1] + (1,))
    p = jax.nn.softmax(jnp.concatenate([s, s_sink], axis=-1), axis=-1)[..., :-1].astype(v.dtype)
    return jnp.einsum('bhgqk,bkhd->bqhgd', p, v).reshape(bn, n, A_HEADS * d)


def window_gqa_latent(q, k, v, kc, vc, sink):
    bn, s_len, _, d = q.shape
    nb = s_len // A_BLOCK
    grp = A_HEADS // A_KV_HEADS
    scale = d ** -0.5
    pad = [(0, 0), (A_BLOCK, A_BLOCK), (0, 0), (0, 0)]

    def band(t):
        tb = jnp.pad(t, pad).reshape(bn, nb + 2, A_BLOCK, A_KV_HEADS, d)
        w = jnp.concatenate([tb[:, :-2], tb[:, 1:-1], tb[:, 2:]], axis=2)
        return jnp.moveaxis(w, 1, 0)

    kw, vw = band(k), band(v)
    qb = jnp.moveaxis(q.reshape(bn, nb, A_BLOCK, A_KV_HEADS, grp, d), 1, 0)
    qi = jnp.arange(A_BLOCK)[:, None]
    kj = jnp.arange(3 * A_BLOCK)[None, :] - A_BLOCK
    band_ok = jnp.abs(kj - qi) <= A_WINDOW
    sink_g = sink.astype(jnp.float32).reshape(A_KV_HEADS, grp)[None, :, :, None, None]

    def block(args):
        n, qn, kn, vn = args
        kpos = n * A_BLOCK + kj
        mask = band_ok & (kpos >= 0) & (kpos < s_len)
        s_loc = jnp.einsum('bqhgd,bkhd->bhgqk', qn, kn).astype(jnp.float32) * scale
        s_loc = jnp.where(mask, s_loc, -jnp.inf)
        s_ctx = jnp.einsum('bqhgd,blhd->bhgql', qn, kc).astype(jnp.float32) * scale
        s_sink = jnp.broadcast_to(sink_g, s_loc.shape[:-1] + (1,))
        p = jax.nn.softmax(jnp.concatenate([s_loc, s_ctx, s_sink], axis=-1), axis=-1)
        p_loc = p[..., :3 * A_BLOCK].astype(vn.dtype)
        p_ctx = p[..., 3 * A_BLOCK:-1].astype(vc.dtype)
        return (jnp.einsum('bhgqk,bkhd->bqhgd', p_loc, vn)
                + jnp.einsum('bhgql,blhd->bqhgd', p_ctx, vc))

    o = lax.map(block, (jnp.arange(nb), qb, kw, vw))
    return jnp.moveaxis(o, 0, 1).reshape(bn, s_len, A_HEADS * d)


def mla_dense(qn, qr, kn, kr, v):
    scale = (B_NOPE + B_ROPE) ** -0.5
    s = (jnp.einsum('bqhd,bkhd->bhqk', qn, kn) + jnp.einsum('bqhr,bkr->bhqk', qr, kr)).astype(jnp.float32) * scale
    p = jax.nn.softmax(s, axis=-1).astype(v.dtype)
    return jnp.einsum('bhqk,bkhd->bqhd', p, v)


def mla_latent(qn, qr, kn, kr, v, kn_c, kr_c, v_c):
    bn, s_len = qn.shape[0], qn.shape[1]
    nb = s_len // B_BLOCK
    k_n = jnp.concatenate([kn_c, kn], axis=1)
    k_r = jnp.concatenate([kr_c, kr], axis=1)
    v_all = jnp.concatenate([v_c, v], axis=1)
    qn_b = jnp.moveaxis(qn.reshape(bn, nb, B_BLOCK, B_HEADS, B_NOPE), 1, 0)
    qr_b = jnp.moveaxis(qr.reshape(bn, nb, B_BLOCK, B_HEADS, B_ROPE), 1, 0)
    o = lax.map(lambda a: mla_dense(a[0], a[1], k_n, k_r, v_all), (qn_b, qr_b))
    return jnp.moveaxis(o, 0, 1).reshape(bn, s_len, B_HEADS * B_V)


def ab_project(u, in_w, qn_g, w_uq, kvn_g, w_ukv):
    bn, n, _ = u.shape
    p = u @ in_w
    qa, ka, va, za, cq, ckv, kr, zb = jnp.split(p, AB_SPLITS, axis=-1)
    qa = qa.reshape(bn, n, A_HEADS, HEAD_DIM)
    ka = ka.reshape(bn, n, A_KV_HEADS, HEAD_DIM)
    va = va.reshape(bn, n, A_KV_HEADS, HEAD_DIM)
    qb = (rms_norm(cq, qn_g) @ w_uq).reshape(bn, n, B_HEADS, B_NOPE + B_ROPE)
    kvb = (rms_norm(ckv, kvn_g) @ w_ukv).reshape(bn, n, B_HEADS, B_NOPE + B_V)
    return (qa, ka, va, za, qb[..., :B_NOPE], qb[..., B_NOPE:],
            kvb[..., :B_NOPE], kr, kvb[..., B_NOPE:], zb)


def ab_mixer(u, uc, in_w, out_w, sink, qn_g, w_uq, kvn_g, w_ukv, rope_a, rope_b, need_ctx):
    qa, ka, va, za, qbn, qbr, kbn, kbr, vb, zb = ab_project(u, in_w, qn_g, w_uq, kvn_g, w_ukv)
    qa_c, ka_c, va_c, za_c, qbn_c, qbr_c, kbn_c, kbr_c, vb_c, zb_c = ab_project(uc, in_w, qn_g, w_uq, kvn_g, w_ukv)
    cos_a, sin_a = rope_a
    cos_b, sin_b = rope_b
    qa = apply_rope(qa, cos_a[:, None, :], sin_a[:, None, :])
    ka = apply_rope(ka, cos_a[:, None, :], sin_a[:, None, :])
    qbr = apply_rope(qbr, cos_b[:, None, :], sin_b[:, None, :])
    kbr = apply_rope(kbr, cos_b, sin_b)
    oa = window_gqa_latent(qa, ka, va, ka_c, va_c, sink)
    ob = mla_latent(qbn, qbr, kbn, kbr, vb, kbn_c, kbr_c, vb_c)
    y = jnp.concatenate([oa * jax.nn.silu(za), ob * jax.nn.silu(zb)], axis=-1) @ out_w
    if not need_ctx:
        return y, None
    bn, n = uc.shape[0], uc.shape[1]
    oa_c = gqa_sink_dense(qa_c, ka_c, va_c, sink)
    ob_c = mla_dense(qbn_c, qbr_c, kbn_c, kbr_c, vb_c).reshape(bn, n, B_WIDTH)
    yc = jnp.concatenate([oa_c * jax.nn.silu(za_c), ob_c * jax.nn.silu(zb_c)], axis=-1) @ out_w
    return y, yc


def mha_dense(q, k, v):
    bn, n, h, d = q.shape
    s = jnp.einsum('bqhd,bkhd->bhqk', q, k).astype(jnp.float32) * (d ** -0.5)
    p = jax.nn.softmax(s, axis=-1).astype(v.dtype)
    return jnp.einsum('bhqk,bkhd->bqhd', p, v).reshape(bn, n, h * d)


def neighbourhood_latent(q, k, v, kc, vc, rpb):
    bn, s_len, h, d = q.shape
    rows = s_len // GRID_W
    kh = min(C_KH, rows)
    scale = d ** -0.5
    qg = jnp.moveaxis(q.reshape(bn, rows, GRID_W, h, d), 1, 0)
    kg = k.reshape(bn, rows, GRID_W, h, d)
    vg = v.reshape(bn, rows, GRID_W, h, d)
    col = jnp.arange(GRID_W)
    cs = jnp.clip(col - C_KW // 2, 0, GRID_W - C_KW)
    col_ok = (col[None, :] >= cs[:, None]) & (col[None, :] < cs[:, None] + C_KW)
    col_idx = jnp.clip(col[None, :] - col[:, None] + (C_KW - 1), 0, 2 * C_KW - 2)
    rpb_col = rpb[:, :, col_idx]
    mask = jnp.broadcast_to(col_ok[:, None, :], (GRID_W, kh, GRID_W)).reshape(GRID_W, kh * GRID_W)

    def row_block(args):
        r, qr = args
        rs = jnp.clip(r - kh // 2, 0, rows - kh)
        kr = lax.dynamic_slice_in_dim(kg, rs, kh, axis=1).reshape(bn, kh * GRID_W, h, d)
        vr = lax.dynamic_slice_in_dim(vg, rs, kh, axis=1).reshape(bn, kh * GRID_W, h, d)
        row_idx = rs + jnp.arange(kh) - r + (C_KH - 1)
        bias = jnp.take(rpb_col, row_idx, axis=1)
        bias = jnp.transpose(bias, (0, 2, 1, 3)).reshape(h, GRID_W, kh * GRID_W)
        s_loc = jnp.einsum('bqhd,bkhd->bhqk', qr, kr).astype(jnp.float32) * scale + bias.astype(jnp.float32)
        s_loc = jnp.where(mask, s_loc, -jnp.inf)
        s_ctx = jnp.einsum('bqhd,blhd->bhql', qr, kc).astype(jnp.float32) * scale
        p = jax.nn.softmax(jnp.concatenate([s_loc, s_ctx], axis=-1), axis=-1)
        p_loc = p[..., :kh * GRID_W].astype(vr.dtype)
        p_ctx = p[..., kh * GRID_W:].astype(vc.dtype)
        return jnp.einsum('bhqk,bkhd->bqhd', p_loc, vr) + jnp.einsum('bhql,blhd->bqhd', p_ctx, vc)

    o = lax.map(row_block, (jnp.arange(rows), qg))
    return jnp.moveaxis(o, 0, 1).reshape(bn, s_len, h * d)


def c_mixer(u, uc, in_w, out_w, rpb, need_ctx):
    def proj(t):
        bn, n, _ = t.shape
        q, k, v, z = jnp.split(t @ in_w, 4, axis=-1)
        shp = (bn, n, C_HEADS, HEAD_DIM)
        return q.reshape(shp), k.reshape(shp), v.reshape(shp), z
    q, k, v, z = proj(u)
    q_c, k_c, v_c, z_c = proj(uc)
    o = neighbourhood_latent(q, k, v, k_c, v_c, rpb)
    y = (o * jax.nn.silu(z)) @ out_w
    if not need_ctx:
        return y, None
    yc = (mha_dense(q_c, k_c, v_c) * jax.nn.silu(z_c)) @ out_w
    return y, yc


def setup_inputs(seed: int = 0) -> dict:
    key = jax.random.key(seed)
    ks = jax.random.split(key, 20)
    f32 = jnp.float32

    def w(k, shape, fan_in, gain=1.0):
        return jax.random.normal(k, shape, f32) * (gain * fan_in ** -0.5)

    def gain(k, shape):
        return 1.0 + 0.05 * jax.random.normal(k, shape, f32)

    return {
        "x": jax.random.normal(ks[0], (BATCH, SEQ, D_MODEL), f32),
        "c": jax.random.normal(ks[1], (BATCH, D_MODEL), f32),
        "ctx": jax.random.normal(ks[2], (BATCH, CTX_LEN, D_MODEL), f32),
        "c_ctx": jax.random.normal(ks[3], (D_MODEL,), f32),
        "ada_w": w(ks[4], (DEPTH, D_MODEL, 3 * D_MODEL), D_MODEL, 0.5),
        "ada_b": 0.02 * jax.random.normal(ks[5], (DEPTH, 3 * D_MODEL), f32),
        "norm_g": gain(ks[6], (DEPTH, D_MODEL)),
        "ab_in_w": w(ks[7], (N_EVEN, D_MODEL, AB_IN), D_MODEL),
        "ab_out_w": w(ks[8], (N_EVEN, AB_MIX, D_MODEL), AB_MIX),
        "a_sink": jax.random.normal(ks[9], (N_EVEN, A_HEADS), f32),
        "b_q_norm_g": gain(ks[10], (N_EVEN, B_Q_LORA)),
        "b_w_uq": w(ks[11], (N_EVEN, B_Q_LORA, B_HEADS * (B_NOPE + B_ROPE)), B_Q_LORA),
        "b_kv_norm_g": gain(ks[12], (N_EVEN, B_KV_LORA)),
        "b_w_ukv": w(ks[13], (N_EVEN, B_KV_LORA, B_HEADS * (B_NOPE + B_V)), B_KV_LORA),
        "c_in_w": w(ks[14], (N_ODD, D_MODEL, C_IN), D_MODEL),
        "c_out_w": w(ks[15], (N_ODD, C_WIDTH, D_MODEL), C_WIDTH),
        "c_rpb": 0.5 * jax.random.normal(ks[16], (N_ODD, C_HEADS, 2 * C_KH - 1, 2 * C_KW - 1), f32),
        "final_g": gain(ks[17], (D_MODEL,)),
    }


def reference(x, c, ctx, c_ctx, ada_w, ada_b, norm_g, ab_in_w, ab_out_w, a_sink,
              b_q_norm_g, b_w_uq, b_kv_norm_g, b_w_ukv, c_in_w, c_out_w, c_rpb, final_g):
    n_lat = x.shape[1]
    rope_a = axial_rope_angles(n_lat, HEAD_DIM)
    rope_b = axial_rope_angles(n_lat, B_ROPE)
    sc = jax.nn.silu(c)
    sc_ctx = jax.nn.silu(c_ctx)
    h, hc = x, ctx
    for layer in range(DEPTH):
        last = layer == DEPTH - 1
        mod = sc @ ada_w[layer] + ada_b[layer]
        shift, scale, gate = jnp.split(mod[:, None, :], 3, axis=-1)
        mod_c = sc_ctx @ ada_w[layer] + ada_b[layer]
        shift_c, scale_c, gate_c = jnp.split(mod_c, 3, axis=-1)
        u = rms_norm(h, norm_g[layer]) * (1 + scale) + shift
        uc = rms_norm(hc, norm_g[layer]) * (1 + scale_c) + shift_c
        if layer % 2 == 0:
            i = layer // 2
            y, yc = ab_mixer(u, uc, ab_in_w[i], ab_out_w[i], a_sink[i], b_q_norm_g[i], b_w_uq[i],
                             b_kv_norm_g[i], b_w_ukv[i], rope_a, rope_b, not last)
        else:
            i = layer // 2
            y, yc = c_mixer(u, uc, c_in_w[i], c_out_w[i], c_rpb[i], not last)
        h = h + gate * y
        if not last:
            hc = hc + gate_c * yc
    return rms_norm(h, final_g)
```

```python
import numpy as np
from contextlib import ExitStack
import concourse.bass as bass
import concourse.mybir as mybir
from concourse.bass_utils import run_bass_kernel_spmd

F32 = mybir.dt.float32
BF16 = mybir.dt.bfloat16
AF = mybir.ActivationFunctionType
ALU = mybir.AluOpType

D = 1024
TL = 4096
TC = 256
TA = TL + TC
NEG = -30000.0
EPS = 1e-6
PAIRS = [[0, 1], [2, 3], [4, 5], [6, 7]]

O_QA, O_QAS, O_KA, O_KAS, O_VA, O_ZA, O_CQ, O_CKV, O_KR, O_KRS, O_ZB = (
    0, 512, 1024, 1152, 1280, 1408, 1920, 2304, 2560, 2592, 2624)
NC_AB = 3136


class Buf:
    __slots__ = ("w", "r", "multi")

    def __init__(self, multi=False):
        self.w = {}
        self.r = {}
        self.multi = multi


class Chan:
    __slots__ = ("key", "sem", "cnt")


class KB:
    def __init__(self):
        self.nc = bass.Bass("TRN2", target_bir_lowering=False)
        nc = self.nc
        self.es = ExitStack()
        self.E = dict(pe=nc.tensor, act=nc.scalar, dve=nc.vector, pool=nc.gpsimd, sp=nc.sync)
        self.sems = {}
        self.ecnt = {}
        self.known = {e: {} for e in self.E}
        for e in self.E:
            self.sems["e_" + e] = self.es.enter_context(nc.semaphore("e_" + e))
            self.ecnt[e] = 0
        self.chans = {}
        self.nins = 0

    def chan(self, name):
        if name not in self.chans:
            c = Chan()
            c.key = "c_" + name
            c.sem = self.es.enter_context(self.nc.semaphore(c.key))
            c.cnt = 0
            self.sems[c.key] = c.sem
            self.chans[name] = c
        return self.chans[name]

    def sb(self, st, name, shape, dt):
        self.nsb = getattr(self, "nsb", 0) + 1
        return st.enter_context(self.nc.sbuf_tensor("sb%d_%s" % (self.nsb, name), list(shape), dt))

    def dram(self, name, shape, dt, kind=None):
        if kind:
            return self.nc.dram_tensor(name, list(shape), dt, kind=kind).ap()
        return self.nc.dram_tensor(name, list(shape), dt).ap()

    def _cur(self, key):
        if key.startswith("e_"):
            return None
        return self.chans[key[2:]].cnt

    def deps(self, e, reads, writes):
        need = {}

        def add(ev):
            k, v = ev
            if need.get(k, 0) < v:
                need[k] = v
        for b in reads:
            for k, v in b.w.items():
                add((k, v))
        for b in writes:
            if not b.multi:
                for k, v in b.w.items():
                    add((k, v))
            for k, v in b.r.items():
                add((k, v))
        kn = self.known[e]
        for k, v in need.items():
            if k.startswith("c_"):
                v = max(v, self.chans[k[2:]].cnt)
            if kn.get(k, 0) < v:
                self.E[e].wait_ge(self.sems[k], v)
                kn[k] = v
                self.nins += 1

    def record(self, ev, reads, writes):
        k, v = ev
        for b in writes:
            if b.multi:
                if b.w.get(k, 0) < v:
                    b.w[k] = v
            else:
                b.w = {k: v}
            b.r = {}
        for b in reads:
            if b.r.get(k, 0) < v:
                b.r[k] = v

    def op(self, e, fn, reads=(), writes=()):
        self.deps(e, reads, writes)
        ins = fn(self.E[e])
        self.ecnt[e] += 1
        ins.then_inc(self.sems["e_" + e], 1)
        self.record(("e_" + e, self.ecnt[e]), reads, writes)
        self.nins += 1

    def dma(self, ch, out, in_, reads=(), writes=(), e="sp"):
        self.deps(e, reads, writes)
        ins = self.E[e].dma_start(out=out, in_=in_)
        ch.cnt += 16
        ins.then_inc(ch.sem, 16)
        self.record((ch.key, ch.cnt), reads, writes)
        self.nins += 1

    def finish(self, bufs):
        self.deps("sp", bufs, bufs)


def _swap_pairs(w):
    idx = np.arange(w.shape[1]).reshape(-1, 2)[:, ::-1].reshape(-1)
    return w[:, idx]


def _rope_tables(rot_dim, pos):
    ppa = rot_dim // 4
    inv = (10000.0 ** (-np.arange(ppa, dtype=np.float32) / ppa)).astype(np.float32)
    row = (pos // 64).astype(np.float32)
    col = (pos % 64).astype(np.float32)
    ang = np.concatenate([row[:, None] * inv, col[:, None] * inv], axis=-1).astype(np.float32)
    c = np.cos(ang).astype(np.float32)
    s = np.sin(ang).astype(np.float32)
    C = np.repeat(c, 2, axis=1).T
    S = np.stack([-s, s], axis=-1).reshape(len(pos), rot_dim).T
    return np.ascontiguousarray(C), np.ascontiguousarray(S)


def _bf16_dt():
    import ml_dtypes
    return ml_dtypes.bfloat16


def _a_masks(s):
    j = np.arange(128)[:, None]
    i = np.arange(128)[None, :]
    bandL = np.where(j >= i, 0.0, NEG).astype(np.float32)
    bandR = np.where(j <= i, 0.0, NEG).astype(np.float32)
    haloL = bandL if s == 1 else np.full((128, 128), NEG, np.float32)
    haloR = bandR if s == 0 else np.full((128, 128), NEG, np.float32)
    m = np.stack([bandL, bandR, haloL, haloR], 0)
    m = np.tile(m[:, :, None, :], (1, 1, 4, 1)).reshape(4, 128, 512)
    return np.ascontiguousarray(m.transpose(1, 0, 2))


def c_pair_chunks(P):
    if P == 0:
        cs = list(range(0, 6))
    elif P == 1:
        cs = list(range(1, 6))
    elif P == 30:
        cs = list(range(30, 35))
    elif P == 31:
        cs = list(range(30, 36))
    else:
        cs = list(range(P, P + 5))
    return [(c, 2 * c - 4 - 2 * P) for c in cs]


def c_val_index():
    idx = {}
    n = 5
    for P in range(32):
        for ci, (c, Dd) in enumerate(c_pair_chunks(P)):
            if 2 <= P <= 29:
                idx[(P, c)] = ci
            else:
                idx[(P, c)] = n
                n += 1
    return idx, n


def _c_valid(s):
    idx, n = c_val_index()
    out = np.zeros((n, 128, 128), np.float32)
    done = set()
    for P in range(32):
        for (c, Dd) in c_pair_chunks(P):
            t = idx[(P, c)]
            if t in done:
                continue
            done.add(t)
            for ph in range(2):
                for qh in range(2):
                    e = 2 * c + ph
                    r = 2 * P + qh
                    Rk = 64 * s + e - 4
                    R = 64 * s + r
                    rs = min(max(R - 4, 0), 120)
                    ok = (0 <= Rk <= 127) and (rs <= Rk < rs + 8)
                    out[t, ph * 64:(ph + 1) * 64, qh * 64:(qh + 1) * 64] = 0.0 if ok else NEG
    out = np.tile(out[:, :, None, :], (1, 1, 4, 1)).reshape(n, 128, 512)
    return np.ascontiguousarray(out.transpose(1, 0, 2))


def _c_gtab(rpb):
    col = np.arange(64)
    cs = np.clip(col - 8, 0, 48)
    col_ok = (col[None, :] >= cs[:, None]) & (col[None, :] < cs[:, None] + 16)
    cidx = np.clip(col[None, :] - col[:, None] + 15, 0, 30)
    G = np.full((7, 128, 16, 128), NEG, np.float32)
    for di, Dd in enumerate(range(-6, 7, 2)):
        for ph in range(2):
            for qh in range(2):
                dr = Dd + ph - qh
                if dr < -7 or dr > 7:
                    continue
                blk = rpb[:, dr + 7, :][:, cidx]
                blk = np.where(col_ok[None], blk, NEG)
                G[di, ph * 64:(ph + 1) * 64, :, qh * 64:(qh + 1) * 64] = blk.transpose(2, 0, 1)
    return G


def host_inputs(inp, core):
    b, s = core // 2, core % 2
    f = np.float32
    g = lambda k: np.asarray(inp[k], dtype=f)
    x, ctx = g("x"), g("ctx")
    m = {}
    h0 = np.concatenate([x[b, s * TL:(s + 1) * TL, :], ctx[b]], 0)
    m["h0T"] = np.ascontiguousarray(h0.T)
    c2 = np.stack([g("c")[b], g("c_ctx")], -1)
    m["c2T"] = np.ascontiguousarray(c2.reshape(8, 128, 2).transpose(1, 0, 2))
    m["ada_w"] = g("ada_w")
    m["adabT"] = np.ascontiguousarray(g("ada_b").reshape(4, 24, 128).transpose(2, 0, 1))
    m["normgT"] = np.ascontiguousarray(g("norm_g").reshape(4, 8, 128).transpose(2, 0, 1))
    m["finalgT"] = np.ascontiguousarray(g("final_g").reshape(8, 128).T)
    w = g("ab_in_w")
    wab = np.zeros((2, D, NC_AB), f)
    for i in range(2):
        qa, ka, va, za, cq, ckv, kr, zb = np.split(w[i], [512, 640, 768, 1280, 1664, 1920, 1952], axis=1)
        wab[i] = np.concatenate([qa, _swap_pairs(qa), ka, _swap_pairs(ka), va, za, cq, ckv, kr, _swap_pairs(kr), zb], 1)
    m["w_ab"] = wab
    wuq = g("b_w_uq").reshape(2, 384, 8, 96)
    nope = wuq[..., :64].reshape(2, 384, 512)
    rope = wuq[..., 64:].reshape(2, 384, 256)
    rope_s = np.stack([_swap_pairs(rope[i]) for i in range(2)], 0)
    m["w_uq"] = np.ascontiguousarray(np.concatenate([nope, rope, rope_s], -1))
    wukv = g("b_w_ukv").reshape(2, 256, 8, 128)
    m["w_ukv"] = np.ascontiguousarray(np.concatenate([wukv[..., :64].reshape(2, 256, 512), wukv[..., 64:].reshape(2, 256, 512)], -1))
    m["ab_out_w"] = g("ab_out_w")
    m["qngT"] = np.ascontiguousarray(g("b_q_norm_g").reshape(2, 3, 128).transpose(2, 0, 1))
    m["kvngT"] = np.ascontiguousarray(g("b_kv_norm_g").reshape(2, 2, 128).transpose(2, 0, 1))
    m["sink"] = np.ascontiguousarray(g("a_sink").reshape(1, 16))
    m["c_in_w"] = g("c_in_w")
    m["c_out_w"] = g("c_out_w")
    rpb = g("c_rpb")
    m["c_gtab"] = np.stack([_c_gtab(rpb[i]) for i in range(2)], 0)
    m["c_val"] = _c_valid(s)
    m["a_mask"] = _a_masks(s)
    pos = np.arange(s * TL, (s + 1) * TL)
    ca, sa = _rope_tables(64, pos)
    m["ropeA"] = np.stack([np.concatenate([ca, ca], 0), np.concatenate([sa, sa], 0)], 0)
    cb, sbb = _rope_tables(32, pos)
    m["ropeB"] = np.stack([np.tile(cb, (4, 1)), np.tile(sbb, (4, 1))], 0)
    m["ident"] = np.eye(128, dtype=f)
    return m


INPUT_SHAPES = None


class Prog(KB):
    def __init__(self, shapes, nlayers=4, debug=()):
        super().__init__()
        self.debug = debug
        self.nlayers = nlayers
        self.I = {k: self.dram(k, v, F32, kind="ExternalInput") for k, v in shapes.items()}
        self.bufI = {k: Buf() for k in shapes}
        self.outT = self.dram("outT", [D, TL], F32, kind="ExternalOutput")
        self.b_out = Buf()
        self.scr = {}
        self.bscr = {}
        st = self.es
        nc = self.nc
        self.ps = [st.enter_context(nc.psum_tensor("ps%d" % i, [128, 512], F32)) for i in range(8)]
        self.bps = [Buf() for _ in range(8)]
        self.ident = self.sb(st, "ident", [128, 128], BF16)
        self.identf = self.sb(st, "identf", [128, 128], F32)
        self.onesf = self.sb(st, "onesf", [128, 128], F32)
        self.b_const = Buf()
        self.modAB = self.sb(st, "modAB", [128, 4, 2, 3, 8], F32)
        self.b_mod = Buf()
        self.esink = self.sb(st, "esink", [1, 16, 128], BF16)
        self.sel = self.sb(st, "sel", [1, 128], BF16)
        self.qng = self.sb(st, "qng", [128, 2, 3], F32)
        self.kvng = self.sb(st, "kvng", [128, 2, 2], F32)
        self.fing = self.sb(st, "fing", [128, 8], F32)
        self.epsb = self.sb(st, "epsb", [128, 1], F32)

    def S(self, name, shape, dt):
        if name not in self.scr:
            self.scr[name] = self.dram("s_" + name, shape, dt)
            self.bscr[name] = Buf()
        return self.scr[name], self.bscr[name]

    def dump(self, name, ap, buf, shape, dt):
        if name in self.debug:
            o = self.dram("dbg_" + name, shape, dt, kind="ExternalOutput")
            bo = Buf()
            self.dma(self.chan("dbg"), o, ap, reads=[buf], writes=[bo])
            self.dbg_bufs.append(bo)

    def prologue(self):
        nc = self.nc
        I, bI = self.I, self.bufI
        self.dbg_bufs = []
        with ExitStack() as st:
            ch = self.chan("pro")
            chw = [self.chan("prow0"), self.chan("prow1")]
            c2 = self.sb(st, "c2", [128, 8, 2], F32)
            sc = self.sb(st, "sc", [128, 8, 2], F32)
            adab = self.sb(st, "adab", [128, 4, 24], F32)
            ng = self.sb(st, "ng", [128, 4, 8], F32)
            modT = self.sb(st, "modT", [128, 4, 24, 2], F32)
            sinkt = self.sb(st, "sinkt", [1, 16], F32)
            esf = self.sb(st, "esf", [1, 16], F32)
            wst = self.sb(st, "wst", [128, 2, 8, 512], F32)
            b_c2, b_sc, b_adab, b_ng, b_modT, b_sink, b_esf = (Buf() for _ in range(7))
            b_w = [Buf(), Buf()]
            bc = self.b_const
            self.dma(ch, self.identf[:], I["ident"], writes=[bc])
            self.dma(ch, c2[:], I["c2T"], writes=[b_c2])
            self.dma(ch, adab[:], I["adabT"], writes=[b_adab])
            self.dma(ch, ng[:], I["normgT"], writes=[b_ng])
            self.dma(ch, sinkt[:], I["sink"], writes=[b_sink])
            self.dma(ch, self.qng[:], I["qngT"], writes=[bc])
            self.dma(ch, self.kvng[:], I["kvngT"], writes=[bc])
            self.dma(ch, self.fing[:], I["finalgT"], writes=[bc])
            self.op("dve", lambda v: v.tensor_copy(out=self.ident[:], in_=self.identf[:]), reads=[bc], writes=[bc])
            self.op("dve", lambda v: v.memset(self.onesf[:], 1.0), writes=[bc])
            self.op("dve", lambda v: v.memset(self.epsb[:], EPS), writes=[bc])
            self.op("dve", lambda v: v.memset(self.sel[:, 0:64], 0.0), writes=[bc])
            self.op("dve", lambda v: v.memset(self.sel[:, 64:128], 1.0), writes=[bc])
            self.op("act", lambda a: a.activation(out=sc[:], in_=c2[:], func=AF.Silu), reads=[b_c2], writes=[b_sc])
            self.op("act", lambda a: a.activation(out=esf[:], in_=sinkt[:], func=AF.Exp), reads=[b_sink], writes=[b_esf])
            self.op("dve", lambda v: v.tensor_copy(out=self.esink[:], in_=esf[:].unsqueeze(2).to_broadcast([1, 16, 128])),
                    reads=[b_esf], writes=[bc])
            wv = I["ada_w"].rearrange("l (k p) c -> l p k c", p=128)
            it = 0
            for l in range(self.nlayers):
                for cg in range(6):
                    sl = it % 2
                    it += 1
                    self.dma(chw[sl], wst[:, sl], wv[l, :, :, cg * 512:(cg + 1) * 512], writes=[b_w[sl]])
                    bank = self.ps[sl]
                    bb = self.bps[sl]

                    def mm(pe, sl=sl, bank=bank):
                        ins = None
                        for j in range(4):
                            for k in range(8):
                                ins = pe.matmul(bank[:, 2 * j:2 * j + 2], lhsT=wst[:, sl, k, j * 128:(j + 1) * 128],
                                                rhs=sc[:, k, :], start=(k == 0), stop=(k == 7))
                        return ins
                    self.op("pe", mm, reads=[b_w[sl], b_sc], writes=[bb])
                    for j in range(4):
                        blk = cg * 4 + j
                        self.op("dve", lambda v, j=j, blk=blk, bank=bank, l=l: v.tensor_scalar(
                            out=modT[:, l, blk, :], in0=bank[:, 2 * j:2 * j + 2], scalar1=adab[:, l, blk:blk + 1],
                            scalar2=None, op0=ALU.add), reads=[bb, b_adab], writes=[b_modT])
            for l in range(self.nlayers):
                for t in range(2):
                    self.op("dve", lambda v, l=l, t=t: v.scalar_tensor_tensor(
                        out=self.modAB[:, l, t, 0, :], in0=modT[:, l, 8:16, t], scalar=1.0, in1=ng[:, l, :],
                        op0=ALU.add, op1=ALU.mult), reads=[b_modT, b_ng], writes=[self.b_mod])
                    self.op("dve", lambda v, l=l, t=t: v.tensor_copy(out=self.modAB[:, l, t, 1, :], in_=modT[:, l, 0:8, t]),
                            reads=[b_modT], writes=[self.b_mod])
                    self.op("dve", lambda v, l=l, t=t: v.tensor_copy(out=self.modAB[:, l, t, 2, :], in_=modT[:, l, 16:24, t]),
                            reads=[b_modT], writes=[self.b_mod])
            self.dump("modAB", self.modAB[:], self.b_mod, [128, 4, 2, 3, 8], F32)
            self.quiesce()

    def quiesce(self):
        for e in self.E:
            kn = self.known[e]
            for e2 in self.E:
                k = "e_" + e2
                v = self.ecnt[e2]
                if kn.get(k, 0) < v:
                    self.E[e].wait_ge(self.sems[k], v)
                    kn[k] = v
            for c in self.chans.values():
                if kn.get(c.key, 0) < c.cnt:
                    self.E[e].wait_ge(c.sem, c.cnt)
                    kn[c.key] = c.cnt

    def load_chunk(self, T, src_ap, b_src, t0, n, slot):
        self.dma(self.chan("hT%d" % slot), T["hT"][:, slot, :, :n], src_ap.rearrange("(k p) t -> p k t", p=128)[:, :, t0:t0 + n],
                 reads=[b_src], writes=[T["b_hT"][slot]])

    def norm_chunk(self, st_bufs, src_ap, b_src, t0, n, l, tl, slot):
        T = st_bufs
        hT, uT, sq, rstd, tmp = T["hT"], T["uT"], T["sq"], T["rstd"], T["tmp"]
        bh, bu, bsq, brs, btmp = T["b_hT"][slot], T["b_uT"][slot], T["b_sq"], T["b_rstd"], T["b_tmp"]
        self.op("act", lambda a: a.activation(out=sq[:, :, :n], in_=hT[:, slot, :, :n], func=AF.Square), reads=[bh], writes=[bsq])
        bank, bb = self.ps[7], self.bps[7]

        def mm(pe):
            ins = None
            for k in range(8):
                ins = pe.matmul(bank[:, :n], lhsT=self.onesf[:], rhs=sq[:, k, :n], start=(k == 0), stop=(k == 7))
            return ins
        self.op("pe", mm, reads=[bsq, self.b_const], writes=[bb])
        self.op("act", lambda a: a.activation(out=rstd[:, :n], in_=bank[:, :n], func=AF.Sqrt, scale=1.0 / D, bias=self.epsb[:]),
                reads=[bb, self.b_const], writes=[brs])
        self.op("dve", lambda v: v.reciprocal(out=rstd[:, :n], in_=rstd[:, :n]), reads=[brs], writes=[brs])
        for k in range(8):
            self.op("dve", lambda v, k=k: v.tensor_tensor(out=tmp[:, k % 2, :n], in0=hT[:, slot, k, :n], in1=rstd[:, :n], op=ALU.mult),
                    reads=[bh, brs], writes=[btmp[k % 2]])
            self.op("act", lambda a, k=k: a.activation(out=uT[:, slot, k, :n], in_=tmp[:, k % 2, :n], func=AF.Identity,
                                                       scale=self.modAB[:, l, tl, 0, k:k + 1], bias=self.modAB[:, l, tl, 1, k:k + 1]),
                    reads=[btmp[k % 2], self.b_mod], writes=[bu])

    def p1_alloc(self, st, ncol_w):
        T = {}
        T["W"] = self.sb(st, "W", [128, 8, ncol_w], BF16)
        T["b_W"] = Buf(multi=True)
        T["hT"] = self.sb(st, "hT", [128, 2, 8, 512], F32)
        T["uT"] = self.sb(st, "uT", [128, 2, 8, 512], BF16)
        T["sq"] = self.sb(st, "sq", [128, 8, 512], F32)
        T["rstd"] = self.sb(st, "rstd", [128, 512], F32)
        T["tmp"] = self.sb(st, "tmp", [128, 2, 512], F32)
        T["stb"] = self.sb(st, "stb", [128, 6, 512], BF16)
        T["stf"] = self.sb(st, "stf", [128, 4, 512], F32)
        T["b_hT"] = [Buf(), Buf()]
        T["b_uT"] = [Buf(multi=True), Buf(multi=True)]
        T["b_sq"], T["b_rstd"] = Buf(), Buf()
        T["b_tmp"] = [Buf(), Buf()]
        T["b_stb"] = [Buf() for _ in range(6)]
        T["b_stf"] = [Buf() for _ in range(4)]
        T["i_stb"] = 0
        T["i_stf"] = 0
        T["i_bank"] = 0
        T["i_ev"] = 0
        return T

    def bank(self, T, nb=7):
        i = T["i_bank"] % nb
        T["i_bank"] += 1
        return self.ps[i], self.bps[i]

    def stage(self, T, kind):
        if kind == "b":
            i = T["i_stb"] % 6
            T["i_stb"] += 1
            return T["stb"][:, i], T["b_stb"][i], self.chan("stb%d" % i)
        i = T["i_stf"] % 4
        T["i_stf"] += 1
        return T["stf"][:, i], T["b_stf"][i], self.chan("stf%d" % i)

    def ev_engine(self, T):
        T["i_ev"] += 1
        return "act" if T["i_ev"] % 2 else "dve"

    def load_w(self, W, bW, src, nk, name):
        for k in range(nk):
            self.dma(self.chan("%s%d" % (name, k % 2)), W[:, k, :], src[k * 128:(k + 1) * 128, :], writes=[bW], e="pool")

    def proj_fm(self, T, W, bW, c0, M, xT, bx, nk, n):
        bank, bb = self.bank(T)

        def mm(pe):
            ins = None
            for k in range(nk):
                ins = pe.matmul(bank[:M, :n], lhsT=W[:, k, c0:c0 + M], rhs=xT[:, k, :n], start=(k == 0), stop=(k == nk - 1))
            return ins
        self.op("pe", mm, reads=[bW, bx], writes=[bb])
        return bank, bb

    def proj_tm(self, T, W, bW, c0, N, xT, bx, nk, s0):
        bank, bb = self.bank(T)

        def mm(pe):
            ins = None
            for k in range(nk):
                ins = pe.matmul(bank[:, :N], lhsT=xT[:, k, s0:s0 + 128], rhs=W[:, k, c0:c0 + N], start=(k == 0), stop=(k == nk - 1))
            return ins
        self.op("pe", mm, reads=[bW, bx], writes=[bb])
        return bank, bb

    def evac_store(self, T, bank, bb, M, n, dsts, kind="b", func=None, scale=1.0):
        stg, bs, ch = self.stage(T, kind)
        if func is not None or scale != 1.0:
            f = func if func is not None else AF.Copy
            self.op("act", lambda a: a.activation(out=stg[:M, :n], in_=bank[:M, :n], func=f, scale=scale), reads=[bb], writes=[bs])
        else:
            e = self.ev_engine(T)
            if e == "act":
                self.op("act", lambda a: a.copy(out=stg[:M, :n], in_=bank[:M, :n]), reads=[bb], writes=[bs])
            else:
                self.op("dve", lambda v: v.tensor_copy(out=stg[:M, :n], in_=bank[:M, :n]), reads=[bb], writes=[bs])
        for (ap, bd, p0, p1) in dsts:
            self.dma(ch, ap, stg[p0:p1, :n], reads=[bs], writes=[bd])

    def rope_store(self, T, bank, bb, banks, bbs, M, n, Ct, St, brope, dsts):
        stg, bs, ch = self.stage(T, "b")
        t1, b1, _ = self.stage(T, "f")
        t2, b2, _ = self.stage(T, "f")
        self.op("dve", lambda v: v.tensor_tensor(out=t1[:M, :n], in0=bank[:M, :n], in1=Ct[:M, :n], op=ALU.mult), reads=[bb, brope], writes=[b1])
        self.op("dve", lambda v: v.tensor_tensor(out=t2[:M, :n], in0=banks[:M, :n], in1=St[:M, :n], op=ALU.mult), reads=[bbs, brope], writes=[b2])
        self.op("pool", lambda g: g.tensor_tensor(out=stg[:M, :n], in0=t1[:M, :n], in1=t2[:M, :n], op=ALU.add), reads=[b1, b2], writes=[bs])
        for (ap, bd, p0, p1) in dsts:
            self.dma(ch, ap, stg[p0:p1, :n], reads=[bs], writes=[bd])

    def chunks(self, need_ctx):
        cs = [(c * 512, 512, 0) for c in range(8)]
        if need_ctx is not None:
            cs = [(TL, TC, 1)] + cs
        return cs

    def lat_norm(self, T, xin, bx, nk, n, g_ap, out_bf, b_out):
        sq, bsq, rstd, brs = T["sq"], T["b_sq"], T["rstd"], T["b_rstd"]
        self.op("act", lambda a: a.activation(out=sq[:, :nk, :n], in_=xin[:, :nk, :n], func=AF.Square), reads=[bx], writes=[bsq])
        bank, bb = self.ps[7], self.bps[7]

        def mm(pe):
            ins = None
            for k in range(nk):
                ins = pe.matmul(bank[:, :n], lhsT=self.onesf[:], rhs=sq[:, k, :n], start=(k == 0), stop=(k == nk - 1))
            return ins
        self.op("pe", mm, reads=[bsq, self.b_const], writes=[bb])
        self.op("act", lambda a: a.activation(out=rstd[:, :n], in_=bank[:, :n], func=AF.Sqrt, scale=1.0 / (128 * nk), bias=self.epsb[:]),
                reads=[bb, self.b_const], writes=[brs])
        self.op("dve", lambda v: v.reciprocal(out=rstd[:, :n], in_=rstd[:, :n]), reads=[brs], writes=[brs])
        for k in range(nk):
            self.op("dve", lambda v, k=k: v.scalar_tensor_tensor(out=out_bf[:, k, :n], in0=xin[:, k, :n], scalar=g_ap[:, k:k + 1], in1=rstd[:, :n],
                                                                 op0=ALU.mult, op1=ALU.mult), reads=[bx, brs, self.b_const], writes=[b_out])

    def phase1_ab(self, l, src, b_src, need_ctx):
        i = l // 2
        I = self.I
        with ExitStack() as st:
            T = self.p1_alloc(st, NC_AB)
            W, bW = T["W"], T["b_W"]
            Wuq = self.sb(st, "Wuq", [128, 3, 1024], BF16)
            Wukv = self.sb(st, "Wukv", [128, 2, 1024], BF16)
            bWuq, bWukv = Buf(multi=True), Buf(multi=True)
            rA = self.sb(st, "rA", [128, 2, 2, 512], F32)
            rB = self.sb(st, "rB", [128, 2, 2, 512], F32)
            brA, brB = [Buf(), Buf()], [Buf(), Buf()]
            cq = self.sb(st, "cq", [128, 3, 512], F32)
            cqn = self.sb(st, "cqn", [128, 3, 512], BF16)
            ckv = self.sb(st, "ckv", [128, 2, 512], F32)
            ckvn = self.sb(st, "ckvn", [128, 2, 512], BF16)
            bcq, bcqn, bckv, bckvn = Buf(multi=True), Buf(multi=True), Buf(multi=True), Buf(multi=True)
            self.load_w(W, bW, I["w_ab"][i], 8, "w")
            self.load_w(Wuq, bWuq, I["w_uq"][i], 3, "w")
            self.load_w(Wukv, bWukv, I["w_ukv"][i], 2, "w")
            QA, bQA = self.S("QA_T", [512, TA], BF16)
            KAL, bKAL = self.S("KAL_T", [128, TL], BF16)
            KAC, bKAC = self.S("KAC_T", [128, TC], BF16)
            VAL, bVAL = self.S("VAL", [TL, 128], BF16)
            VAC, bVAC = self.S("VAC", [TC, 128], BF16)
            Z, bZ = self.S("Z_T", [1024, TA], F32)
            QB, bQB = self.S("QB_T", [768, TA], BF16)
            KBLs = [self.S("KBL_T%d" % hh, [96, TL], BF16) for hh in range(8)]
            VBLs = [self.S("VBL%d" % q, [1024, 512], BF16) for q in range(4)]
            for (_, bb_) in KBLs + VBLs:
                bb_.multi = True
            KBC, bKBC = self.S("KBC_T", [768, TC], BF16)
            VBC, bVBC = self.S("VBC", [TC, 512], BF16)
            for b in (bQA, bKAL, bKAC, bVAL, bVAC, bZ, bQB, bKBC, bVBC):
                b.multi = True
            cl = self.chunks(True)
            self.load_chunk(T, src, b_src, cl[0][0], cl[0][1], 0)
            for ci, (t0, n, tl) in enumerate(cl):
                slot = ci % 2
                lat = (tl == 0)
                if ci + 1 < len(cl):
                    self.load_chunk(T, src, b_src, cl[ci + 1][0], cl[ci + 1][1], 1 - slot)
                self.norm_chunk(T, src, b_src, t0, n, l, tl, slot)
                uT, bu = T["uT"][:, slot], T["b_uT"][slot]
                if lat:
                    self.dma(self.chan("rA%d" % slot), rA[:, slot], I["ropeA"].rearrange("c p t -> p c t")[:, :, t0:t0 + n], writes=[brA[slot]])
                    self.dma(self.chan("rB%d" % slot), rB[:, slot], I["ropeB"].rearrange("c p t -> p c t")[:, :, t0:t0 + n], writes=[brB[slot]])
                KA_dst = (lambda r0, r1: (KAL[r0:r1, t0:t0 + n], bKAL)) if lat else (lambda r0, r1: (KAC[r0:r1, 0:n], bKAC))
                KB_dst = (lambda r0, r1: (KBLs[r0 // 96][0][r0 % 96:r0 % 96 + (r1 - r0), t0:t0 + n], KBLs[r0 // 96][1])) if lat else (lambda r0, r1: (KBC[r0:r1, 0:n], bKBC))
                for j in range(5):
                    c0, c0s = (O_QA + j * 128, O_QAS + j * 128) if j < 4 else (O_KA, O_KAS)
                    if j < 4:
                        dst = [(QA[j * 128:(j + 1) * 128, t0:t0 + n], bQA, 0, 128)]
                    else:
                        ap, bd = KA_dst(0, 128)
                        dst = [(ap, bd, 0, 128)]
                    bk, bb = self.proj_fm(T, W, bW, c0, 128, uT, bu, 8, n)
                    if lat:
                        bks, bbs = self.proj_fm(T, W, bW, c0s, 128, uT, bu, 8, n)
                        self.rope_store(T, bk, bb, bks, bbs, 128, n, rA[:, slot, 0], rA[:, slot, 1], brA[slot], dst)
                    else:
                        self.evac_store(T, bk, bb, 128, n, dst)
                for s0 in range(0, n, 128):
                    bk, bb = self.proj_tm(T, W, bW, O_VA, 128, uT, bu, 8, s0)
                    dst = (VAL[t0 + s0:t0 + s0 + 128, :], bVAL) if lat else (VAC[s0:s0 + 128, :], bVAC)
                    self.evac_store(T, bk, bb, 128, 128, [(dst[0], dst[1], 0, 128)])
                for j in range(8):
                    c0 = O_ZA + j * 128 if j < 4 else O_ZB + (j - 4) * 128
                    bk, bb = self.proj_fm(T, W, bW, c0, 128, uT, bu, 8, n)
                    self.evac_store(T, bk, bb, 128, n, [(Z[j * 128:(j + 1) * 128, t0:t0 + n], bZ, 0, 128)], kind="f", func=AF.Silu)
                for j in range(3):
                    bk, bb = self.proj_fm(T, W, bW, O_CQ + j * 128, 128, uT, bu, 8, n)
                    self.op("dve", lambda v, j=j, bk=bk: v.tensor_copy(out=cq[:, j, :n], in_=bk[:, :n]), reads=[bb], writes=[bcq])
                self.lat_norm(T, cq, bcq, 3, n, self.qng[:, i, :], cqn, bcqn)
                for j in range(4):
                    bk, bb = self.proj_fm(T, Wuq, bWuq, j * 128, 128, cqn, bcqn, 3, n)
                    self.evac_store(T, bk, bb, 128, n, [(QB[(2 * j) * 96:(2 * j) * 96 + 64, t0:t0 + n], bQB, 0, 64),
                                                        (QB[(2 * j + 1) * 96:(2 * j + 1) * 96 + 64, t0:t0 + n], bQB, 64, 128)])
                for j in range(2):
                    dst = [(QB[(4 * j + hh) * 96 + 64:(4 * j + hh) * 96 + 96, t0:t0 + n], bQB, hh * 32, hh * 32 + 32) for hh in range(4)]
                    bk, bb = self.proj_fm(T, Wuq, bWuq, 512 + j * 128, 128, cqn, bcqn, 3, n)
                    if lat:
                        bks, bbs = self.proj_fm(T, Wuq, bWuq, 768 + j * 128, 128, cqn, bcqn, 3, n)
                        self.rope_store(T, bk, bb, bks, bbs, 128, n, rB[:, slot, 0], rB[:, slot, 1], brB[slot], dst)
                    else:
                        self.evac_store(T, bk, bb, 128, n, dst)
                for j in range(2):
                    bk, bb = self.proj_fm(T, W, bW, O_CKV + j * 128, 128, uT, bu, 8, n)
                    self.op("dve", lambda v, j=j, bk=bk: v.tensor_copy(out=ckv[:, j, :n], in_=bk[:, :n]), reads=[bb], writes=[bckv])
                self.lat_norm(T, ckv, bckv, 2, n, self.kvng[:, i, :], ckvn, bckvn)
                for j in range(4):
                    bk, bb = self.proj_fm(T, Wukv, bWukv, j * 128, 128, ckvn, bckvn, 2, n)
                    a0, b0 = KB_dst((2 * j) * 96, (2 * j) * 96 + 64)
                    a1, b1 = KB_dst((2 * j + 1) * 96, (2 * j + 1) * 96 + 64)
                    self.evac_store(T, bk, bb, 128, n, [(a0, b0, 0, 64), (a1, b1, 64, 128)])
                for s0 in range(0, n, 128):
                    bk, bb = self.proj_tm(T, Wukv, bWukv, 512, 512, ckvn, bckvn, 2, s0)
                    tq = (t0 + s0) // 1024
                    dst = (VBLs[tq][0][(t0 + s0) % 1024:(t0 + s0) % 1024 + 128, :], VBLs[tq][1]) if lat else (VBC[s0:s0 + 128, :], bVBC)
                    self.evac_store(T, bk, bb, 128, 512, [(dst[0], dst[1], 0, 128)])
                bk, bb = self.proj_fm(T, W, bW, O_KR, 32, uT, bu, 8, n)
                dst = []
                for hh in range(8):
                    a, bd = KB_dst(hh * 96 + 64, hh * 96 + 96)
                    dst.append((a, bd, 0, 32))
                if lat:
                    bks, bbs = self.proj_fm(T, W, bW, O_KRS, 32, uT, bu, 8, n)
                    self.rope_store(T, bk, bb, bks, bbs, 32, n, rB[:, slot, 0], rB[:, slot, 1], brB[slot], dst)
                else:
                    self.evac_store(T, bk, bb, 32, n, dst)
            self.quiesce()

    def phase1_c(self, l, src, b_src, need_ctx):
        i = l // 2
        I = self.I
        with ExitStack() as st:
            T = self.p1_alloc(st, 4096)
            W, bW = T["W"], T["b_W"]
            self.load_w(W, bW, I["c_in_w"][i], 8, "w")
            Q, bQ = self.S("QC_T", [1024, TA], BF16)
            KL, bKL = self.S("KCL_T", [1024, TL], BF16)
            KC, bKC = self.S("KCC_T", [1024, TC], BF16)
            VL, bVL = self.S("VCL", [TL, 1024], BF16)
            VC, bVC = self.S("VCC", [TC, 1024], BF16)
            Z, bZ = self.S("Z_T", [1024, TA], F32)
            sK, bsK = self.S("sndKC", [1024, 512], BF16)
            sV, bsV = self.S("sndVC", [512, 1024], BF16)
            for b in (bQ, bKL, bKC, bVL, bVC, bZ, bsK, bsV):
                b.multi = True
            cl = self.chunks(True)
            self.load_chunk(T, src, b_src, cl[0][0], cl[0][1], 0)
            for ci, (t0, n, tl) in enumerate(cl):
                slot = ci % 2
                lat = (tl == 0)
                if ci + 1 < len(cl):
                    self.load_chunk(T, src, b_src, cl[ci + 1][0], cl[ci + 1][1], 1 - slot)
                self.norm_chunk(T, src, b_src, t0, n, l, tl, slot)
                uT, bu = T["uT"][:, slot], T["b_uT"][slot]
                for j in range(8):
                    bk, bb = self.proj_fm(T, W, bW, j * 128, 128, uT, bu, 8, n)
                    self.evac_store(T, bk, bb, 128, n, [(Q[j * 128:(j + 1) * 128, t0:t0 + n], bQ, 0, 128)], scale=0.125)
                for j in range(8):
                    bk, bb = self.proj_fm(T, W, bW, 1024 + j * 128, 128, uT, bu, 8, n)
                    if lat:
                        dst = [(KL[j * 128:(j + 1) * 128, t0:t0 + n], bKL, 0, 128)]
                    else:
                        dst = [(KC[j * 128:(j + 1) * 128, 0:n], bKC, 0, 128)]
                    stg_before = T["i_stb"]
                    self.evac_store(T, bk, bb, 128, n, dst)
                    if lat and t0 == 0:
                        si = stg_before % 6
                        self.dma(self.chan("stb%d" % si), sK[j * 128:(j + 1) * 128, 0:256], T["stb"][:, si, 0:256], reads=[T["b_stb"][si]], writes=[bsK])
                    if lat and t0 == TL - 512:
                        si = stg_before % 6
                        self.dma(self.chan("stb%d" % si), sK[j * 128:(j + 1) * 128, 256:512], T["stb"][:, si, 256:512], reads=[T["b_stb"][si]], writes=[bsK])
                for s0 in range(0, n, 128):
                    for hf in range(2):
                        bk, bb = self.proj_tm(T, W, bW, 2048 + hf * 512, 512, uT, bu, 8, s0)
                        if lat:
                            dst = [(VL[t0 + s0:t0 + s0 + 128, hf * 512:(hf + 1) * 512], bVL, 0, 128)]
                            if t0 == 0 and s0 < 256:
                                dst.append((sV[s0:s0 + 128, hf * 512:(hf + 1) * 512], bsV, 0, 128))
                            if t0 == TL - 512 and s0 >= 256:
                                dst.append((sV[s0:s0 + 128, hf * 512:(hf + 1) * 512], bsV, 0, 128))
                        else:
                            dst = [(VC[s0:s0 + 128, hf * 512:(hf + 1) * 512], bVC, 0, 128)]
                        self.evac_store(T, bk, bb, 128, 512, dst)
                for j in range(8):
                    bk, bb = self.proj_fm(T, W, bW, 3072 + j * 128, 128, uT, bu, 8, n)
                    self.evac_store(T, bk, bb, 128, n, [(Z[j * 128:(j + 1) * 128, t0:t0 + n], bZ, 0, 128)], kind="f", func=AF.Silu)
            self.quiesce()

    def gather(self, src, bsrc, name, shape, dt):
        dst, bdst = self.S(name, shape, dt)
        self.op("pool", lambda g: g.collective_compute("AllGather", ALU.bypass, replica_groups=PAIRS, ins=[src], outs=[dst]),
                reads=[bsrc], writes=[bdst])
        return dst, bdst

    def attn_alloc(self, st):
        A = {}
        A["pT"] = self.sb(st, "pT", [128, 3, 512], BF16)
        A["b_pT"] = [Buf() for _ in range(3)]
        A["rl"] = self.sb(st, "rl", [128, 2, 512], F32)
        A["o"] = self.sb(st, "o", [64, 2, 512], F32)
        A["z"] = self.sb(st, "z", [64, 2, 512], F32)
        A["mx"] = self.sb(st, "mx", [64, 2, 512], BF16)
        A["b_rl"], A["b_o"], A["b_z"], A["b_mx"] = ([Buf(), Buf()] for _ in range(4))
        A["i_s"] = 0
        A["i_o"] = 0
        A["i_f"] = 0
        return A

    def attn_stream(self, A, tiles, scale):
        items = []
        for ti, t in enumerate(tiles):
            t["obank"] = 3 + (A["i_o"] % 3)
            A["i_o"] += 1
            for ci, c in enumerate(t["chunks"]):
                items.append((t, c, ci == 0, ci == len(t["chunks"]) - 1))

        def emit_score(t, c):
            if c.get("sink") is not None:
                return
            si = A["i_s"] % 3
            A["i_s"] += 1
            c["si"] = si
            bank, bb = self.ps[si], self.bps[si]
            N = t["N"]

            def mm(pe):
                ins = None
                first = True
                for m in c.get("masks", ()):
                    ins = pe.matmul(bank[:, :N], lhsT=self.ident[:], rhs=m, start=first, stop=False)
                    first = False
                ns = len(c["score"])
                for i2, (lh, rh, c0, ncol) in enumerate(c["score"]):
                    ins = pe.matmul(bank[:, c0:c0 + ncol], lhsT=lh, rhs=rh, start=first, stop=(i2 == ns - 1))
                    first = False
                return ins
            self.op("pe", mm, reads=list(c["reads"]) + [self.b_const], writes=[bb])

        def emit_rest(t, c, firstc, lastc):
            N = t["N"]
            ob, bob = self.ps[t["obank"]], self.bps[t["obank"]]
            if c.get("sink") is not None:
                self.op("pe", lambda pe: pe.matmul(ob[:, :N], lhsT=self.sel[:], rhs=c["sink"], start=firstc, stop=lastc),
                        reads=[self.b_const], writes=[bob])
            else:
                si = c["si"]
                bank, bb = self.ps[si], self.bps[si]
                pT, bp = A["pT"][:, si], A["b_pT"][si]
                self.op("act", lambda a: a.activation(out=pT[:, :N], in_=bank[:, :N], func=AF.Exp, scale=scale), reads=[bb], writes=[bp])

                def pv(pe):
                    ins = None
                    npv = len(c["pv"])
                    for i2, (lh, c0, ncol) in enumerate(c["pv"]):
                        ins = pe.matmul(ob[:, c0:c0 + ncol], lhsT=lh, rhs=pT[:, c0:c0 + ncol], start=(firstc and i2 == 0),
                                        stop=(lastc and i2 == npv - 1))
                    return ins
                self.op("pe", pv, reads=[bp] + list(c["reads"]), writes=[bob])
            if lastc:
                t["fin"](ob, bob, N)

        if not items:
            return
        emit_score(items[0][0], items[0][1])
        for k, (t, c, f, la) in enumerate(items):
            if k + 1 < len(items):
                emit_score(items[k + 1][0], items[k + 1][1])
            emit_rest(t, c, f, la)

    def attn_fin(self, A, ob, bob, N, z_src, bZ, mix_dst, bMIX, view=None):
        s = A["i_f"] % 2
        A["i_f"] += 1
        rl, o, z, mx = A["rl"][:, s], A["o"][:, s], A["z"][:, s], A["mx"][:, s]
        brl, bo, bz, bmx = A["b_rl"][s], A["b_o"][s], A["b_z"][s], A["b_mx"][s]
        zt = z[:, :N] if view is None else z[:, :N].rearrange("d (i t) -> d i t", i=view)
        mt = mx[:, :N] if view is None else mx[:, :N].rearrange("d (i t) -> d i t", i=view)
        self.dma(self.chan("z%d" % s), zt, z_src, reads=[bZ], writes=[bz])
        self.op("dve", lambda v: v.reciprocal(out=rl[64:128, :N], in_=ob[64:128, :N]), reads=[bob], writes=[brl])
        self.op("dve", lambda v: v.tensor_tensor(out=o[0:64, :N], in0=ob[0:64, :N], in1=rl[64:128, :N], op=ALU.mult), reads=[bob, brl], writes=[bo])
        self.op("pool", lambda g: g.tensor_tensor(out=mx[:, :N], in0=o[:, :N], in1=z[:, :N], op=ALU.mult), reads=[bo, bz], writes=[bmx])
        self.dma(self.chan("mx%d" % s), mix_dst, mt, reads=[bmx], writes=[bMIX], e="pool")

    def attn_a(self, l, need_ctx):
        i = l // 2
        I = self.I
        S = self.scr
        bS = self.bscr
        KAG, bKAG = self.gather(S["KAL_T"], bS["KAL_T"], "KAG_T", [256, TL], BF16)
        VAG, bVAG = self.gather(S["VAL"], bS["VAL"], "VAG", [2 * TL, 128], BF16)
        Z, bZ = S["Z_T"], bS["Z_T"]
        MIX, bMIX = self.S("MIX_T", [1024, TA], BF16)
        bMIX.multi = True
        with ExitStack() as st:
            A = self.attn_alloc(st)
            K = [self.sb(st, "KAs%d" % g, [64, 36 * 128], BF16) for g in range(2)]
            Q = [self.sb(st, "QAs%d" % g, [64, 4, TA], BF16) for g in range(2)]
            V = self.sb(st, "VAs", [128, 36, 2, 2, 64], BF16)
            M = self.sb(st, "Msk", [128, 4, 512], BF16)
            bK, bQ, bV, bM = Buf(multi=True), Buf(multi=True), Buf(multi=True), Buf()
            ch = self.chan("akv")
            self.dma(self.chan("amsk"), M[:], I["a_mask"], writes=[bM], e="pool")
            self.op("dve", lambda v: v.memset(V[:, :, :, 1, :], 1.0), writes=[bV])
            for g in range(2):
                r0 = g * 64
                self.dma(ch, K[g][:, 0:128], KAG[r0:r0 + 64, TL - 128:TL], reads=[bKAG], writes=[bK])
                self.dma(ch, K[g][:, 128:128 + TL], S["KAL_T"][r0:r0 + 64, :], reads=[bS["KAL_T"]], writes=[bK])
                self.dma(ch, K[g][:, 33 * 128:34 * 128], KAG[128 + r0:128 + r0 + 64, 0:128], reads=[bKAG], writes=[bK])
                self.dma(ch, K[g][:, 34 * 128:36 * 128], S["KAC_T"][r0:r0 + 64, :], reads=[bS["KAC_T"]], writes=[bK])
                self.dma(ch, Q[g][:], S["QA_T"][g * 256:(g + 1) * 256, :].rearrange("(i d) t -> d i t", d=64), reads=[bS["QA_T"]], writes=[bQ])
            for g in range(2):
                gs = slice(g * 64, (g + 1) * 64)
                self.dma(ch, V[:, 0, g, 0, :], VAG[TL - 128:TL, gs], reads=[bVAG], writes=[bV])
                for q2 in range(2):
                    self.dma(ch, V[:, 1 + q2 * 16:17 + q2 * 16, g, 0, :], S["VAL"][q2 * 2048:(q2 + 1) * 2048, gs].rearrange("(c p) d -> p c d", p=128),
                             reads=[bS["VAL"]], writes=[bV])
                self.dma(ch, V[:, 33, g, 0, :], VAG[TL:TL + 128, gs], reads=[bVAG], writes=[bV])
                self.dma(ch, V[:, 34:36, g, 0, :], S["VAC"][:, gs].rearrange("(c p) d -> p c d", p=128), reads=[bS["VAC"]], writes=[bV])
            tiles = []
            for g in range(2):
                sink = self.esink[0:1, i * 8 + g * 4:i * 8 + g * 4 + 4, :]
                nblk = 34 if need_ctx else 32
                for n in range(nblk):
                    rhs = Q[g][:, :, n * 128:(n + 1) * 128]
                    if n < 32:
                        kl = [(n, 2 if n == 0 else 0), (n + 1, None), (n + 2, 3 if n == 31 else 1), (34, None), (35, None)]
                    else:
                        kl = [(34, None), (35, None)]
                    chunks = []
                    for (kc, mi) in kl:
                        chunks.append(dict(score=[(K[g][:, kc * 128:(kc + 1) * 128], rhs, 0, 512)],
                                           masks=[M[:, mi, :]] if mi is not None else [],
                                           pv=[(V[:, kc, g].rearrange("p a d -> p (a d)"), 0, 512)], reads=[bK, bQ, bV, bM]))
                    chunks.append(dict(sink=sink, reads=[]))
                    rows = slice(g * 256, (g + 1) * 256)
                    cols = slice(n * 128, (n + 1) * 128)
                    zs = Z[rows, cols].rearrange("(i d) t -> d i t", d=64)
                    md = MIX[rows, cols].rearrange("(i d) t -> d i t", d=64)
                    tiles.append(dict(N=512, chunks=chunks,
                                      fin=lambda ob, bob, N, zs=zs, md=md: self.attn_fin(A, ob, bob, N, zs, bZ, md, bMIX, view=4)))
            self.attn_stream(A, tiles, 0.125)
            self.quiesce()

    def attn_b(self, l, need_ctx):
        S, bS = self.scr, self.bscr
        KBGs = [self.gather(S["KBL_T%d" % hh], bS["KBL_T%d" % hh], "KBG_T%d" % hh, [192, TL], BF16) for hh in range(8)]
        VBGs = [self.gather(S["VBL%d" % q], bS["VBL%d" % q], "VBG%d" % q, [2048, 512], BF16) for q in range(4)]
        Z, bZ = S["Z_T"], bS["Z_T"]
        MIX, bMIX = self.S("MIX_T", [1024, TA], BF16)
        bMIX.multi = True
        with ExitStack() as st:
            A = self.attn_alloc(st)
            NK = 66
            K = self.sb(st, "KBs", [96, 2, NK * 128], BF16)
            Q = self.sb(st, "QBs", [96, 2, TA], BF16)
            V = self.sb(st, "VBs", [128, 2, NK, 2, 64], BF16)
            bK, bQ = [Buf(multi=True), Buf(multi=True)], [Buf(), Buf()]
            bV = [Buf(multi=True), Buf(multi=True)]
            self.op("dve", lambda v: v.memset(V[:, 0, :, 1, :], 1.0), writes=[bV[0]])
            self.op("dve", lambda v: v.memset(V[:, 1, :, 1, :], 1.0), writes=[bV[1]])
            scale = float(96 ** -0.5)
            import os
            for h in range(int(os.environ.get('DBG_BH', '8'))):
                sl = h % 2
                chk = self.chan("bk%d" % sl)
                KBG, bKBG = KBGs[h]
                self.dma(chk, K[:, sl, 0:TL], KBG[0:96, :], reads=[bKBG], writes=[bK[sl]])
                self.dma(chk, K[:, sl, TL:2 * TL], KBG[96:192, :], reads=[bKBG], writes=[bK[sl]])
                self.dma(chk, K[:, sl, 2 * TL:2 * TL + TC], S["KBC_T"][h * 96:(h + 1) * 96, :], reads=[bS["KBC_T"]], writes=[bK[sl]])
                self.dma(self.chan("bq%d" % sl), Q[:, sl, :], S["QB_T"][h * 96:(h + 1) * 96, :], reads=[bS["QB_T"]], writes=[bQ[sl]])
                chv = self.chan("bv%d" % sl)
                for r in range(2):
                    for q4 in range(4):
                        VBG, bVBG = VBGs[q4]
                        self.dma(chv, V[:, sl, r * 32 + q4 * 8:r * 32 + q4 * 8 + 8, 0, :],
                                 VBG[r * 1024:(r + 1) * 1024, h * 64:(h + 1) * 64].rearrange("(c p) d -> p c d", p=128),
                                 reads=[bVBG], writes=[bV[sl]])
                self.dma(chv, V[:, sl, 64:66, 0, :], S["VBC"][:, h * 64:(h + 1) * 64].rearrange("(c p) d -> p c d", p=128), reads=[bS["VBC"]], writes=[bV[sl]])
                tiles = []
                qt = [(c * 512, 512, list(range(NK))) for c in range(8)]
                if need_ctx:
                    qt.append((TL, TC, [64, 65]))
                for (t0, N, kcs) in qt:
                    rhs = Q[:, sl, t0:t0 + N]
                    chunks = [dict(score=[(K[:, sl, kc * 128:(kc + 1) * 128], rhs, 0, N)], pv=[(V[:, sl, kc].rearrange("p a d -> p (a d)"), 0, N)],
                                   reads=[bK[sl], bQ[sl], bV[sl]]) for kc in kcs]
                    rows = slice(512 + h * 64, 512 + (h + 1) * 64)
                    zs = Z[rows, t0:t0 + N]
                    md = MIX[rows, t0:t0 + N]
                    tiles.append(dict(N=N, chunks=chunks, fin=lambda ob, bob, N, zs=zs, md=md: self.attn_fin(A, ob, bob, N, zs, bZ, md, bMIX)))
                self.attn_stream(A, tiles, scale)
            self.quiesce()

    def attn_c(self, l, need_ctx):
        i = l // 2
        I = self.I
        S, bS = self.scr, self.bscr
        rK, brK = self.gather(S["sndKC"], bS["sndKC"], "rcvKC", [2048, 512], BF16)
        rV, brV = self.gather(S["sndVC"], bS["sndVC"], "rcvVC", [1024, 1024], BF16)
        Z, bZ = S["Z_T"], bS["Z_T"]
        MIX, bMIX = self.S("MIX_T", [1024, TA], BF16)
        bMIX.multi = True
        vidx, nv = c_val_index()
        with ExitStack() as st:
            A = self.attn_alloc(st)
            NKc = 38
            K = self.sb(st, "KCs", [64, 4, NKc * 128], BF16)
            Q = self.sb(st, "QCs", [64, 4, TA], BF16)
            V = self.sb(st, "VCs", [128, NKc, 4, 2, 64], BF16)
            G = self.sb(st, "Gs", [128, 7, 4, 128], BF16)
            CV = self.sb(st, "CVs", [128, nv, 512], BF16)
            bK, bQ, bV, bG, bCV = Buf(multi=True), Buf(multi=True), Buf(multi=True), Buf(multi=True), Buf()
            self.dma(self.chan("ccv"), CV[:], I["c_val"], writes=[bCV], e="pool")
            ch = self.chan("ckv")
            for hg in range(4):
                r0 = hg * 256
                rr = slice(r0, r0 + 256)
                self.op("dve", lambda v: v.memset(V[:, :, :, 1, :], 1.0), writes=[bV])
                self.dma(ch, K[:, :, 0:256], rK[r0:r0 + 256, 256:512].rearrange("(i d) t -> d i t", d=64), reads=[brK], writes=[bK])
                self.dma(ch, K[:, :, 256:256 + TL], S["KCL_T"][rr, :].rearrange("(i d) t -> d i t", d=64), reads=[bS["KCL_T"]], writes=[bK])
                self.dma(ch, K[:, :, 256 + TL:512 + TL], rK[1024 + r0:1024 + r0 + 256, 0:256].rearrange("(i d) t -> d i t", d=64), reads=[brK], writes=[bK])
                self.dma(ch, K[:, :, 512 + TL:512 + TL + TC], S["KCC_T"][rr, :].rearrange("(i d) t -> d i t", d=64), reads=[bS["KCC_T"]], writes=[bK])
                self.dma(ch, Q[:], S["QC_T"][rr, :].rearrange("(i d) t -> d i t", d=64), reads=[bS["QC_T"]], writes=[bQ])
                for hh in range(4):
                    hs = slice(r0 + hh * 64, r0 + (hh + 1) * 64)
                    self.dma(ch, V[:, 0:2, hh, 0, :], rV[256:512, hs].rearrange("(c p) d -> p c d", p=128), reads=[brV], writes=[bV])
                    for q2 in range(2):
                        self.dma(ch, V[:, 2 + q2 * 16:18 + q2 * 16, hh, 0, :],
                                 S["VCL"][q2 * 2048:(q2 + 1) * 2048, hs].rearrange("(c p) d -> p c d", p=128), reads=[bS["VCL"]], writes=[bV])
                    self.dma(ch, V[:, 34:36, hh, 0, :], rV[512:768, hs].rearrange("(c p) d -> p c d", p=128), reads=[brV], writes=[bV])
                    self.dma(ch, V[:, 36:38, hh, 0, :], S["VCC"][:, hs].rearrange("(c p) d -> p c d", p=128), reads=[bS["VCC"]], writes=[bV])
                for di in range(7):
                    self.dma(self.chan("cg"), G[:, di], I["c_gtab"][i, di, :, hg * 4:(hg + 1) * 4, :], writes=[bG], e="pool")
                tiles = []
                nblk = 34 if need_ctx else 32
                for P in range(nblk):
                    cols = slice(P * 128, (P + 1) * 128)
                    if P < 32:
                        kl = [(c, (Dd + 6) // 2, vidx[(P, c)]) for (c, Dd) in c_pair_chunks(P)] + [(36, None, None), (37, None, None)]
                    else:
                        kl = [(36, None, None), (37, None, None)]
                    chunks = []
                    for (kc, di, vi) in kl:
                        masks = [] if di is None else [G[:, di].rearrange("p h q -> p (h q)"), CV[:, vi, :]]
                        chunks.append(dict(score=[(K[:, hh, kc * 128:(kc + 1) * 128], Q[:, hh, cols], hh * 128, 128) for hh in range(4)],
                                           masks=masks, pv=[(V[:, kc, hh].rearrange("p a d -> p (a d)"), hh * 128, 128) for hh in range(4)],
                                           reads=[bK, bQ, bV, bG, bCV]))
                    zs = Z[rr, cols].rearrange("(i d) t -> d i t", d=64)
                    md = MIX[rr, cols].rearrange("(i d) t -> d i t", d=64)
                    tiles.append(dict(N=512, chunks=chunks,
                                      fin=lambda ob, bob, N, zs=zs, md=md: self.attn_fin(A, ob, bob, N, zs, bZ, md, bMIX, view=4)))
                self.attn_stream(A, tiles, 1.0)
            self.quiesce()

    def phase4(self, l, wsrc, hsrc, b_hsrc, hdst, b_hdst, need_ctx, last):
        S, bS = self.scr, self.bscr
        MIX, bMIX = S["MIX_T"], bS["MIX_T"]
        with ExitStack() as st:
            W = self.sb(st, "Wo", [128, 8, 1024], BF16)
            bW = Buf(multi=True)
            self.load_w(W, bW, wsrc, 8, "w")
            mx = self.sb(st, "mxi", [128, 2, 8, 512], BF16)
            hT = self.sb(st, "hTi", [128, 2, 8, 512], F32)
            hn = self.sb(st, "hn", [128, 2, 8, 512], F32)
            sq = self.sb(st, "sq4", [128, 8, 512], F32)
            rstd = self.sb(st, "rstd4", [128, 512], F32)
            bmx, bh, bhn = [Buf(), Buf()], [Buf(), Buf()], [Buf(multi=True), Buf(multi=True)]
            bsq, brs = Buf(), Buf()
            cs = self.chunks(True if need_ctx else None)
            ib = 0
            for ci, (t0, n, tl) in enumerate(cs):
                s = ci % 2
                self.dma(self.chan("p4m%d" % s), mx[:, s, :, :n], MIX.rearrange("(k p) t -> p k t", p=128)[:, :, t0:t0 + n], reads=[bMIX], writes=[bmx[s]])
                self.dma(self.chan("p4h%d" % s), hT[:, s, :, :n], hsrc.rearrange("(k p) t -> p k t", p=128)[:, :, t0:t0 + n], reads=[b_hsrc], writes=[bh[s]])
                for j in range(8):
                    bank, bb = self.ps[ib % 6], self.bps[ib % 6]
                    ib += 1

                    def mm(pe, bank=bank, j=j):
                        ins = None
                        for k in range(8):
                            ins = pe.matmul(bank[:, :n], lhsT=W[:, k, j * 128:(j + 1) * 128], rhs=mx[:, s, k, :n], start=(k == 0), stop=(k == 7))
                        return ins
                    self.op("pe", mm, reads=[bW, bmx[s]], writes=[bb])
                    self.op("dve", lambda v, bank=bank, j=j: v.scalar_tensor_tensor(
                        out=hn[:, s, j, :n], in0=bank[:, :n], scalar=self.modAB[:, l, tl, 2, j:j + 1], in1=hT[:, s, j, :n],
                        op0=ALU.mult, op1=ALU.add), reads=[bb, bh[s], self.b_mod], writes=[bhn[s]])
                if not last:
                    self.dma(self.chan("p4o%d" % s), hdst.rearrange("(k p) t -> p k t", p=128)[:, :, t0:t0 + n], hn[:, s, :, :n],
                             reads=[bhn[s]], writes=[b_hdst], e="pool")
                else:
                    self.op("act", lambda a: a.activation(out=sq[:, :, :n], in_=hn[:, s, :, :n], func=AF.Square), reads=[bhn[s]], writes=[bsq])
                    bank, bb = self.ps[7], self.bps[7]

                    def mm2(pe, bank=bank):
                        ins = None
                        for k in range(8):
                            ins = pe.matmul(bank[:, :n], lhsT=self.onesf[:], rhs=sq[:, k, :n], start=(k == 0), stop=(k == 7))
                        return ins
                    self.op("pe", mm2, reads=[bsq, self.b_const], writes=[bb])
                    self.op("act", lambda a, bank=bank: a.activation(out=rstd[:, :n], in_=bank[:, :n], func=AF.Sqrt, scale=1.0 / D, bias=self.epsb[:]),
                            reads=[bb, self.b_const], writes=[brs])
                    self.op("dve", lambda v: v.reciprocal(out=rstd[:, :n], in_=rstd[:, :n]), reads=[brs], writes=[brs])
                    for k in range(8):
                        self.op("dve", lambda v, k=k: v.scalar_tensor_tensor(out=hn[:, s, k, :n], in0=hn[:, s, k, :n], scalar=self.fing[:, k:k + 1],
                                                                             in1=rstd[:, :n], op0=ALU.mult, op1=ALU.mult),
                                reads=[brs, self.b_const], writes=[bhn[s]])
                    self.dma(self.chan("p4o%d" % s), self.outT.rearrange("(k p) t -> p k t", p=128)[:, :, t0:t0 + n], hn[:, s, :, :n],
                             reads=[bhn[s]], writes=[self.b_out], e="pool")
            self.quiesce()

    def build(self, stop_after=None):
        self.prologue()
        hs, bhs = self.S("hT", [D, TA], F32)
        bhs.multi = True
        self.b_out.multi = True
        src, bsrc = self.I["h0T"], self.bufI["h0T"]
        for l in range(self.nlayers):
            last = (l == 3)
            need_ctx = not last
            if l % 2 == 0:
                self.phase1_ab(l, src, bsrc, need_ctx)
                if stop_after == ("p1", l):
                    break
                self.attn_a(l, need_ctx)
                if stop_after == ("attn_a", l):
                    break
                self.attn_b(l, need_ctx)
                w = self.I["ab_out_w"][l // 2]
            else:
                self.phase1_c(l, src, bsrc, need_ctx)
                if stop_after == ("p1", l):
                    break
                self.attn_c(l, need_ctx)
                w = self.I["c_out_w"][l // 2]
            if stop_after == ("attn", l):
                break
            self.phase4(l, w, src, bsrc, hs, bhs, need_ctx, last)
            src, bsrc = hs, bhs
            if stop_after == ("p4", l):
                break
        for name in self.debug:
            if name in self.scr:
                sh = list(self.scr[name].shape)
                o = self.dram("dbg_" + name, sh, self.scr[name].dtype, kind="ExternalOutput")
                bo = Buf()
                self.dma(self.chan("dbg"), o, self.scr[name], reads=[self.bscr[name]], writes=[bo])
                self.dbg_bufs.append(bo)
        self.quiesce()
        self.es.close()
        return self.nc


_SHAPES = None


def kernel(**inputs):
    ins = [host_inputs(inputs, c) for c in range(8)]
    shapes = {k: list(v.shape) for k, v in ins[0].items()}
    p = Prog(shapes)
    nc = p.build()
    res = run_bass_kernel_spmd(nc, ins, core_ids=list(range(8)))
    out = np.zeros((4, 2 * TL, D), np.float32)
    for c in range(8):
        b, s = c // 2, c % 2
        out[b, s * TL:(s + 1) * TL, :] = np.asarray(res.results[c]["outT"]).T
    return out
```
